# Optimizing a Trainium2 kernel written in Bass

```python
import jax, jax.numpy as jnp
from jax import lax
import numpy as np

D_MODEL = 2048
BATCH = 4
SEQ = 4096
DEPTH = 2

D_ATTN = D_MODEL // 2
ATTN_HEAD_DIM = 64
N_Q_HEADS = D_ATTN // ATTN_HEAD_DIM
N_KV_HEADS = 4
WINDOW = 128
BLOCK = 128
ROPE_DIM = ATTN_HEAD_DIM // 4
ROPE_THETA = 500000.0

D_RET = D_MODEL - D_ATTN
RET_HEADS = 4
RET_HEAD_DIM = D_RET // RET_HEADS
RET_CHUNK = 128
RET_THETA = 10000.0

D_KV = N_KV_HEADS * ATTN_HEAD_DIM
D_IN = D_ATTN + 2 * D_KV + 4 * D_RET
D_MIX = D_ATTN + D_RET

D_FF = 5632
CONV_W = 3
EPS = 1e-6
NEG_INF = -1e30

kernel_name = "hymba_swa_sink_retention_convffn"


def _rms_norm(x, w):
    xf = x.astype(jnp.float32)
    y = xf * lax.rsqrt(jnp.mean(xf * xf, axis=-1, keepdims=True) + EPS)
    return (y * w.astype(jnp.float32)).astype(x.dtype)


def _rope(x, positions, rot_dim, theta):
    half = rot_dim // 2
    inv = jnp.power(jnp.float32(theta), -jnp.arange(half, dtype=jnp.float32) / half)
    ang = positions.astype(jnp.float32)[..., None] * inv
    cos = jnp.cos(ang)[:, :, None, :].astype(x.dtype)
    sin = jnp.sin(ang)[:, :, None, :].astype(x.dtype)
    x1 = x[..., :half]
    x2 = x[..., half:rot_dim]
    return jnp.concatenate([x1 * cos - x2 * sin, x2 * cos + x1 * sin, x[..., rot_dim:]], axis=-1)


def _swa_sink_attention(q, k, v, sinks):
    B, S, Hq, dh = q.shape
    Hkv = k.shape[2]
    G = Hq // Hkv
    nb = S // BLOCK
    qb = q.reshape(B, nb, BLOCK, Hkv, G, dh)

    def band(t):
        tp = jnp.pad(t, ((0, 0), (BLOCK, 0), (0, 0), (0, 0)))
        tb = tp.reshape(B, nb + 1, BLOCK, Hkv, dh)
        return jnp.concatenate([tb[:, :-1], tb[:, 1:]], axis=2)

    kb = band(k)
    vb = band(v)
    s = jnp.einsum('bnqhgd,bnkhd->bhgnqk', qb, kb).astype(jnp.float32) * (dh ** -0.5)
    qi = jnp.arange(BLOCK)[:, None]
    kj = jnp.arange(2 * BLOCK)[None, :]
    diff = BLOCK + qi - kj
    key_pos = jnp.arange(nb)[:, None, None] * BLOCK + kj[None] - BLOCK
    mask = ((diff >= 0) & (diff < WINDOW))[None] & (key_pos >= 0)
    s = jnp.where(mask, s, NEG_INF)
    sink = sinks.astype(jnp.float32).reshape(Hkv, G)[None, :, :, None, None, None]
    m = jnp.maximum(jnp.max(s, axis=-1, keepdims=True), sink)
    p = jnp.exp(s - m)
    denom = jnp.sum(p, axis=-1, keepdims=True) + jnp.exp(sink - m)
    p = (p / denom).astype(v.dtype)
    o = jnp.einsum('bhgnqk,bnkhd->bnqhgd', p, vb)
    return o.reshape(B, S, Hq * dh)


def _retention(q, k, v):
    B, S, H, d = q.shape
    dv = v.shape[-1]
    nc = S // RET_CHUNK
    q = q.astype(jnp.float32)
    k = k.astype(jnp.float32) * (d ** -0.5)
    v = v.astype(jnp.float32)
    lg = jnp.log(1.0 - jnp.power(2.0, -5.0 - jnp.arange(H, dtype=jnp.float32)))
    idx = jnp.arange(RET_CHUNK, dtype=jnp.float32)
    rel = idx[:, None] - idx[None, :]
    intra = jnp.where(rel >= 0, jnp.exp(lg[:, None, None] * jnp.maximum(rel, 0.0)), 0.0)
    q_dec = jnp.exp(lg[:, None] * (idx + 1.0))[..., None]
    k_dec = jnp.exp(lg[:, None] * (RET_CHUNK - 1.0 - idx))[..., None]
    chunk_dec = jnp.exp(lg * RET_CHUNK)[:, None, None]

    def to_chunks(t):
        return t.reshape(B, nc, RET_CHUNK, H, t.shape[-1]).transpose(1, 0, 3, 2, 4)

    def step(state, xs):
        qc, kc, vc = xs
        inner = jnp.einsum('bhij,bhjd->bhid', jnp.einsum('bhid,bhjd->bhij', qc, kc) * intra, vc)
        cross = jnp.einsum('bhid,bhde->bhie', qc * q_dec, state)
        state = state * chunk_dec + jnp.einsum('bhjd,bhje->bhde', kc * k_dec, vc)
        return state, inner + cross

    state0 = jnp.zeros((B, H, d, dv), jnp.float32)
    _, o = lax.scan(step, state0, (to_chunks(q), to_chunks(k), to_chunks(v)))
    return o.transpose(1, 0, 3, 2, 4).reshape(B, S, H, dv)


def _causal_dwconv(u, w, b):
    S = u.shape[1]
    up = jnp.pad(u, ((0, 0), (CONV_W - 1, 0), (0, 0)))
    y = b.astype(u.dtype)
    for kk in range(CONV_W):
        y = y + up[:, kk:kk + S] * w[kk]
    return y


def setup_inputs(seed: int = 0) -> dict:
    key = jax.random.key(seed)
    ks = jax.random.split(key, 16)
    f32 = jnp.float32
    x = jax.random.normal(ks[0], (BATCH, SEQ, D_MODEL), f32)
    positions = jnp.broadcast_to(jnp.arange(SEQ, dtype=jnp.int32), (BATCH, SEQ))
    w_in = jax.random.normal(ks[1], (DEPTH, D_MODEL, D_IN), f32) * D_MODEL ** -0.5
    w_out = jax.random.normal(ks[2], (DEPTH, D_MIX, D_MODEL), f32) * D_MIX ** -0.5
    w_up = jax.random.normal(ks[3], (DEPTH, D_MODEL, 2 * D_FF), f32) * D_MODEL ** -0.5
    w_down = jax.random.normal(ks[4], (DEPTH, D_FF, D_MODEL), f32) * D_FF ** -0.5
    conv_w = jax.random.normal(ks[5], (DEPTH, CONV_W, 2 * D_FF), f32) * CONV_W ** -0.5
    conv_b = jax.random.normal(ks[6], (DEPTH, 2 * D_FF), f32) * 0.02
    attn_sinks = jax.random.normal(ks[7], (DEPTH, N_Q_HEADS), f32) * 0.5
    def gain(k, shape):
        return 1.0 + 0.02 * jax.random.normal(k, shape, f32)
    pre_mix_norm = gain(ks[8], (DEPTH, D_MODEL))
    post_mix_norm = gain(ks[9], (DEPTH, D_MODEL))
    attn_out_norm = gain(ks[10], (DEPTH, D_ATTN))
    ret_out_norm = gain(ks[11], (DEPTH, RET_HEADS, RET_HEAD_DIM))
    pre_ffn_norm = gain(ks[12], (DEPTH, D_MODEL))
    post_ffn_norm = gain(ks[13], (DEPTH, D_MODEL))
    return {"x": x, "positions": positions, "w_in": w_in, "w_out": w_out,
            "w_up": w_up, "w_down": w_down, "conv_w": conv_w, "conv_b": conv_b,
            "attn_sinks": attn_sinks, "pre_mix_norm": pre_mix_norm,
            "post_mix_norm": post_mix_norm, "attn_out_norm": attn_out_norm,
            "ret_out_norm": ret_out_norm, "pre_ffn_norm": pre_ffn_norm,
            "post_ffn_norm": post_ffn_norm}


def reference(x, positions, w_in, w_out, w_up, w_down, conv_w, conv_b, attn_sinks,
              pre_mix_norm, post_mix_norm, attn_out_norm, ret_out_norm,
              pre_ffn_norm, post_ffn_norm):
    B, S, _ = x.shape
    split_at = [D_ATTN, D_ATTN + D_KV, D_ATTN + 2 * D_KV,
                D_ATTN + 2 * D_KV + D_RET, D_ATTN + 2 * D_KV + 2 * D_RET,
                D_ATTN + 2 * D_KV + 3 * D_RET]
    for l in range(DEPTH):
        h = _rms_norm(x, pre_mix_norm[l])
        proj = h @ w_in[l]
        qa, ka, va, qr, kr, vr, gr = jnp.split(proj, split_at, axis=-1)
        qa = _rope(qa.reshape(B, S, N_Q_HEADS, ATTN_HEAD_DIM), positions, ROPE_DIM, ROPE_THETA)
        ka = _rope(ka.reshape(B, S, N_KV_HEADS, ATTN_HEAD_DIM), positions, ROPE_DIM, ROPE_THETA)
        va = va.reshape(B, S, N_KV_HEADS, ATTN_HEAD_DIM)
        ya = _swa_sink_attention(qa, ka, va, attn_sinks[l])
        ya = _rms_norm(ya, attn_out_norm[l])
        qr = _rope(qr.reshape(B, S, RET_HEADS, RET_HEAD_DIM), positions, RET_HEAD_DIM, RET_THETA)
        kr = _rope(kr.reshape(B, S, RET_HEADS, RET_HEAD_DIM), positions, RET_HEAD_DIM, RET_THETA)
        vr = vr.reshape(B, S, RET_HEADS, RET_HEAD_DIM)
        yr = _rms_norm(_retention(qr, kr, vr), ret_out_norm[l]).astype(x.dtype)
        yr = yr.reshape(B, S, D_RET) * jax.nn.silu(gr)
        mix = jnp.concatenate([ya, yr], axis=-1) @ w_out[l]
        x = x + _rms_norm(mix, post_mix_norm[l])
        h = _rms_norm(x, pre_ffn_norm[l])
        u = _causal_dwconv(h @ w_up[l], conv_w[l], conv_b[l])
        a, g = jnp.split(u, 2, axis=-1)
        f = (jax.nn.gelu(a, approximate=True) * g) @ w_down[l]
        x = x + _rms_norm(f, post_ffn_norm[l])
    return x
```

```python
import math
import os
import numpy as np
from contextlib import ExitStack
import concourse.bass as bass
import concourse.mybir as mybir
from concourse.bass_utils import run_bass_kernel_spmd

F32 = mybir.dt.float32
BF16 = mybir.dt.bfloat16
I32 = mybir.dt.int32
AF = mybir.ActivationFunctionType
ALU = mybir.AluOpType
AX = mybir.AxisListType

D = 2048
D_IN = 5632
D_FF = 5632
EPS = 1e-6
N_CORES = 4
SEQ = 4096
BATCH = 4
DEPTH = 2


class Buf:
    def __init__(self, name, t):
        self.name = name
        self.t = t
        self.w = None
        self.r = {}

    def __getitem__(self, idx):
        return self.t[idx]


class Prog:
    def __init__(self):
        self.nc = bass.Bass("TRN2", target_bir_lowering=False)
        self.es = ExitStack()
        self.engs = ("pe", "act", "dve", "pool", "sp")
        self.lists = {k: [] for k in self.engs}
        self.sems = {}
        self.count = {}
        self.seen = {k: {} for k in self.engs}
        for k in self.engs:
            self._newsem("e_" + k)
        self.scope = None
        self.nins = 0
        self.prefix = ""
        self.nphase = 0

    def _newsem(self, key):
        self.sems[key] = self.es.enter_context(self.nc.semaphore(key))
        self.count[key] = 0
        return key

    def _stack(self):
        return self.scope if self.scope is not None else self.es

    def sb(self, name, shape, dtype):
        return Buf(name, self._stack().enter_context(self.nc.sbuf_tensor(self.prefix + name, list(shape), dtype)))

    def ps(self, name, shape, dtype=F32):
        return Buf(name, self._stack().enter_context(self.nc.psum_tensor(self.prefix + name, list(shape), dtype)))

    def dram(self, name, shape, dtype, kind="Internal"):
        return Buf(name, self.nc.dram_tensor(name, list(shape), dtype, kind=kind).ap())

    def _deps(self, eng, reads, writes):
        need = {}

        def add(d):
            if d is not None and need.get(d[0], 0) < d[1]:
                need[d[0]] = d[1]
        for b in reads:
            add(b.w)
        for b in writes:
            add(b.w)
            for k, v in b.r.items():
                add((k, v))
        waits = []
        for k, v in need.items():
            if k == "e_pe" and eng == "pe":
                continue
            if self.seen[eng].get(k, 0) >= v:
                continue
            self.seen[eng][k] = v
            waits.append((k, v))
        return waits

    def _mark(self, key, val, reads, writes):
        for b in reads:
            if b.r.get(key, 0) < val:
                b.r[key] = val
        for b in writes:
            b.w = (key, val)
            b.r = {}

    def op(self, eng, fn, reads=(), writes=(), inc=True):
        waits = self._deps(eng, reads, writes)
        key = "e_" + eng
        if inc:
            self.count[key] += 1
            val = self.count[key]
        else:
            val = self.count[key] + 1
        self._mark(key, val, reads, writes)
        self.lists[eng].append((waits, fn, (key, 1) if inc else None))
        self.nins += 1

    def dma(self, eng, out_ap, in_ap, reads=(), writes=(), semkey=None, **kw):
        waits = self._deps(eng, reads, writes)
        if semkey is None:
            semkey = "d_" + writes[0].name
        if semkey not in self.sems:
            self._newsem(semkey)
        self.count[semkey] += 16
        val = self.count[semkey]
        self._mark(semkey, val, reads, writes)
        self.lists[eng].append((waits, lambda e: e.dma_start(out=out_ap, in_=in_ap, **kw), (semkey, 16)))
        self.nins += 1

    def begin_phase(self):
        self.scope = ExitStack()
        self.nphase += 1
        self.prefix = "p%d_" % self.nphase

    def end_phase(self):
        waits = []
        for k, v in self.count.items():
            if k.startswith("d_") and self.seen["sp"].get(k, 0) < v:
                self.seen["sp"][k] = v
                waits.append((k, v))
        self.lists["sp"].append((waits, None, None))
        nc, lists, sems = self.nc, self.lists, self.sems
        with nc.Block() as block:
            def mk(engname):
                def body(e):
                    for w, fn, inc in lists[engname]:
                        for k, v in w:
                            e.wait_ge(sems[k], v)
                        if fn is None:
                            continue
                        ins = fn(e)
                        if inc is not None:
                            ins.then_inc(sems[inc[0]], inc[1])
                return body
            block.tensor(mk("pe"))
            block.scalar(mk("act"))
            block.vector(mk("dve"))
            block.gpsimd(mk("pool"))
            block.sync(mk("sp"))
        self.lists = {k: [] for k in self.engs}
        for e in self.engs:
            for k, v in self.count.items():
                if k.startswith("e_"):
                    self.seen[e][k] = v
                elif k.startswith("d_"):
                    self.seen[e][k] = v
        if self.scope is not None:
            self.scope.close()
            self.scope = None
        self.prefix = ""

    def finish(self):
        self.es.close()
        return self.nc


def act(P, out, in_, func, reads, writes, **kw):
    P.op("act", lambda e: e.activation(out=out, in_=in_, func=func, **kw), reads, writes)


def tt(P, eng, out, in0, in1, op, reads, writes):
    P.op(eng, lambda e: e.tensor_tensor(out=out, in0=in0, in1=in1, op=op), reads, writes)


def stt(P, out, in0, scalar, in1, op0, op1, reads, writes):
    P.op("dve", lambda e: e.scalar_tensor_tensor(out=out, in0=in0, scalar=scalar, in1=in1, op0=op0, op1=op1),
         reads, writes)


def ts(P, eng, out, in0, s1, op0, reads, writes, s2=None, op1=None):
    if op1 is None:
        P.op(eng, lambda e: e.tensor_scalar(out=out, in0=in0, scalar1=s1, scalar2=None, op0=op0), reads, writes)
    else:
        P.op(eng, lambda e: e.tensor_scalar(out=out, in0=in0, scalar1=s1, scalar2=s2, op0=op0, op1=op1),
             reads, writes)


def cp(P, eng, out, in_, reads, writes):
    if eng == "act":
        P.op("act", lambda e: e.activation(out=out, in_=in_, func=AF.Copy), reads, writes)
    else:
        P.op(eng, lambda e: e.tensor_copy(out=out, in_=in_), reads, writes)


def mm(P, out, lhsT, rhs, start, stop, reads, writes):
    P.op("pe", lambda e: e.matmul(out=out, lhsT=lhsT, rhs=rhs, start=start, stop=stop), reads, writes, inc=stop)


def tr(P, out, in_, ident, reads, writes, inc=True):
    P.op("pe", lambda e: e.transpose(out=out, in_=in_, identity=ident), reads, writes, inc=inc)


def rstd_from_ss(P, dst, ss, n, eps_t, reads_extra=()):
    k = None
    act(P, dst_ap(dst), dst_ap(ss), AF.Ln, [ss.b, eps_t] + list(reads_extra), [dst.b], scale=1.0 / n, bias=eps_t[:, 0:1])
    act(P, dst_ap(dst), dst_ap(dst), AF.Exp, [dst.b], [dst.b], scale=-0.5)


class View:
    def __init__(self, b, ap):
        self.b = b
        self.ap = ap


def dst_ap(v):
    return v.ap


def build(T, NL=DEPTH, TBM=256, TBF=512, do_ffn=True, dbg=0):
    NCH = T // 128
    P = Prog()
    nc = P.nc
    x_in = P.dram("x", [T, D], F32, "ExternalInput")
    pos_d = P.dram("pos", [128, NCH], I32, "ExternalInput")
    w_in = P.dram("w_in", [DEPTH, D, D_IN], F32, "ExternalInput")
    w_out = P.dram("w_out", [DEPTH, D, D], F32, "ExternalInput")
    w_up = P.dram("w_up", [DEPTH, D, 2 * D_FF], F32, "ExternalInput")
    w_down = P.dram("w_down", [DEPTH, D_FF, D], F32, "ExternalInput")
    convw_d = P.dram("convw", [128, DEPTH, 3, 88], F32, "ExternalInput")
    convb_d = P.dram("convb", [128, DEPTH, 88], F32, "ExternalInput")
    sinks_d = P.dram("sinks", [128, DEPTH, 16], F32, "ExternalInput")
    gpre_d = P.dram("gpre", [128, DEPTH, 3, 16], F32, "ExternalInput")
    gpost_d = P.dram("gpost", [DEPTH, 2, D], F32, "ExternalInput")
    cst_d = P.dram("cst", [128, 1024], F32, "ExternalInput")
    identb_d = P.dram("identb", [128, 128], BF16, "ExternalInput")
    out_d = P.dram("out", [T, D], F32, "ExternalOutput")
    xa_d = P.dram("xa", [T, D], F32)
    xb_d = P.dram("xb", [T, D], F32)

    cst = P.sb("cst_s", [128, 1024], F32)
    identb = P.sb("identb_s", [128, 128], BF16)
    identf = P.sb("identf_s", [128, 128], F32)
    convw = P.sb("convw_s", [128, DEPTH, 3, 88], F32)
    convb = P.sb("convb_s", [128, DEPTH, 88], F32)
    sinks = P.sb("sinks_s", [128, DEPTH, 16], F32)
    gpre = P.sb("gpre_s", [128, DEPTH, 3, 16], F32)
    posf = P.sb("posf", [128, NCH], F32)
    posi = P.sb("posi", [128, NCH], I32)
    eps_t = P.sb("eps_t", [128, 1], F32)
    maskT = P.sb("maskT", [128, 128], F32)

    C_BAND, C_BAND0, C_MASKT, C_INVA, C_INVR, C_DECQ, C_DECK = 0, 256, 512, 640, 648, 776, 780

    P.begin_phase()
    P.dma("sp", cst[:, :], cst_d[:, :], writes=[cst])
    P.dma("sp", identb[:, :], identb_d[:, :], writes=[identb])
    P.dma("sp", convw[:, :, :, :], convw_d[:, :, :, :], writes=[convw])
    P.dma("sp", convb[:, :, :], convb_d[:, :, :], writes=[convb])
    P.dma("sp", sinks[:, :, :], sinks_d[:, :, :], writes=[sinks])
    P.dma("sp", gpre[:, :, :, :], gpre_d[:, :, :, :], writes=[gpre])
    P.dma("sp", posi[:, :], pos_d[:, :], writes=[posi])
    cp(P, "dve", posf[:, :], posi[:, :], [posi], [posf])
    P.op("dve", lambda e: e.memset(eps_t[:, :], EPS), [], [eps_t])
    cp(P, "dve", identf[:, :], identb[:, :], [identb], [identf])
    cp(P, "dve", maskT[:, :], cst[:, C_MASKT:C_MASKT + 128], [cst], [maskT])
    P.end_phase()

    gam = [1.0 - 2.0 ** (-5.0 - h) for h in range(4)]
    g128 = [g ** 128 for g in gam]
    TWO_PI = 2.0 * math.pi
    C1 = 6.28125
    C2 = TWO_PI - C1

    def rope_tables(c0, ncb, inv_col, nf, cosb, sinb, tmp, tmpi, tmp2):
        inv_ap = cst[:, inv_col:inv_col + nf]
        for which, dstb in ((0, sinb), (1, cosb)):
            for j in range(ncb):
                ts(P, "dve", tmp[:, j, 0:nf], inv_ap, posf[:, c0 + j:c0 + j + 1], ALU.mult, [cst, posf], [tmp])
            t_all = tmp[:, 0:ncb, 0:nf]
            i_all = tmpi[:, 0:ncb, 0:nf]
            f_all = tmp2[:, 0:ncb, 0:nf]
            if which == 1:
                ts(P, "dve", t_all, t_all, math.pi / 2.0, ALU.add, [tmp], [tmp])
            ts(P, "dve", i_all, t_all, 1.0 / TWO_PI, ALU.mult, [tmp], [tmpi])
            cp(P, "dve", f_all, i_all, [tmpi], [tmp2])
            stt(P, t_all, f_all, -C1, t_all, ALU.mult, ALU.add, [tmp2, tmp], [tmp])
            stt(P, t_all, f_all, -C2, t_all, ALU.mult, ALU.add, [tmp2, tmp], [tmp])
            ts(P, "dve", f_all, t_all, math.pi, ALU.is_gt, [tmp], [tmp2])
            stt(P, t_all, f_all, -TWO_PI, t_all, ALU.mult, ALU.add, [tmp2, tmp], [tmp])
            ts(P, "dve", f_all, t_all, -math.pi, ALU.is_lt, [tmp], [tmp2])
            stt(P, t_all, f_all, TWO_PI, t_all, ALU.mult, ALU.add, [tmp2, tmp], [tmp])
            act(P, dstb[:, 0:ncb, 0:nf], t_all, AF.Sin, [tmp], [dstb])

    def norm_transpose(xt, gl_idx, l, hT, col0, ps_tr, hb, ss, junk):
        act(P, junk[:, :], xt[:, :], AF.Square, [xt], [junk, ss], accum_out=ss[:, 0:1])
        act(P, ss[:, 0:1], ss[:, 0:1], AF.Ln, [ss, eps_t], [ss], scale=1.0 / D, bias=eps_t[:, 0:1])
        act(P, ss[:, 0:1], ss[:, 0:1], AF.Exp, [ss], [ss], scale=-0.5)
        ts(P, "dve", hb[:, :], xt[:, :], ss[:, 0:1], ALU.mult, [xt, ss], [hb])
        for half in range(2):
            for k in range(8):
                kk = half * 8 + k
                tr(P, ps_tr[:, k * 128:(k + 1) * 128], hb[:, kk * 128:(kk + 1) * 128], identb[:, :],
                   [hb, identb], [ps_tr], inc=(k == 7))
            tt(P, "dve", hT[:, half * 8:half * 8 + 8, col0:col0 + 128],
               ps_tr[:, :].rearrange("p (k t) -> p k t", k=8),
               gpre[:, l, gl_idx, half * 8:half * 8 + 8].unsqueeze(2).broadcast_to([128, 8, 128]),
               ALU.mult, [ps_tr, gpre], [hT])

    def mixer_phase(l, src_d, dst_d):
        P.begin_phase()
        NT = TBM // 128
        NB = T // TBM
        hT = P.sb("m_hT", [128, 16, TBM], BF16)
        wbuf = [P.sb("m_w%d" % i, [128, 16, 512], BF16) for i in range(2)]
        xt = [P.sb("m_x%d" % i, [128, D], F32) for i in range(2)]
        gpost = P.sb("m_gpost", [128, D], F32)
        hb = P.sb("m_hb", [128, D], BF16)
        junk = P.sb("m_junk", [128, D], BF16)
        ss = P.sb("m_ss", [128, 1], F32)
        qa = P.sb("m_qa", [128, NT, 1024], BF16)
        kva = P.sb("m_kva", [128, NT, 512], BF16)
        qr = P.sb("m_qr", [128, NT, 1024], BF16)
        kr = P.sb("m_kr", [128, NT, 1024], BF16)
        vr = P.sb("m_vr", [128, NT, 1024], BF16)
        gr = P.sb("m_gr", [128, NT, 1024], F32)
        cosa = P.sb("m_cosa", [128, NT, 8], F32)
        sina = P.sb("m_sina", [128, NT, 8], F32)
        cosr = P.sb("m_cosr", [128, NT, 128], F32)
        sinr = P.sb("m_sinr", [128, NT, 128], F32)
        rtmp = P.sb("m_rtmp", [128, NT, 128], F32)
        rtmp2 = P.sb("m_rtmp2", [128, NT, 128], F32)
        rtmpi = P.sb("m_rtmpi", [128, NT, 128], I32)
        stg = [P.sb("m_stg%d" % i, [128, 512], F32) for i in range(2)]
        sctr = [0]
        t1 = P.sb("m_t1", [128, 512], F32)
        t2 = P.sb("m_t2", [128, 512], F32)
        t3 = P.sb("m_t3", [128, 512], F32)
        qaT = P.sb("m_qaT", [64, 16, 128], BF16)
        kaT = [P.sb("m_kaT%d" % i, [64, 4, 128], BF16) for i in range(2)]
        va = [P.sb("m_va%d" % i, [128, 256], BF16) for i in range(2)]
        sm = P.sb("m_sm", [128, 4, 257], F32)
        pr = P.sb("m_pr", [128, 4, 257], BF16)
        prT = P.sb("m_prT", [128, 8, 128], BF16)
        negm = P.sb("m_negm", [128, 4], F32)
        den = P.sb("m_den", [128, 4], F32)
        ya = P.sb("m_ya", [128, 1024], F32)
        qkT = P.sb("m_qkT", [128, 16, 128], BF16)
        pT = P.sb("m_pT", [128, 4, 128], BF16)
        U = P.sb("m_U", [128, 8, 256], F32)
        Ub = P.sb("m_Ub", [128, 8, 256], BF16)
        ssr = P.sb("m_ssr", [128, 4], F32)
        sg = P.sb("m_sg", [128, 1024], F32)
        mix = P.sb("m_mix", [128, D], BF16)
        fo = [P.sb("m_fo%d" % i, [128, D], F32) for i in range(NT)]
        ps_tr = P.ps("mp_tr", [128, 1024], BF16)
        ps_mm = [P.ps("mp_mm%d" % i, [128, 512], F32) for i in range(2)]
        ps_s = P.ps("mp_s", [128, 1024], F32)
        ps_o = P.ps("mp_o", [128, 512], F32)
        ps_d = P.ps("mp_d", [128, 512], F32)

        P.dma("sp", gpost[:, :], gpost_d[l, 0, :].partition_broadcast(128), writes=[gpost])
        P.op("dve", lambda e: e.memset(U[:, :, :], 0.0), [], [U])
        P.op("dve", lambda e: e.memset(Ub[:, :, :], 0.0), [], [Ub])

        w_in_v = w_in[l].rearrange("(kc p) n -> p kc n", p=128)
        w_out_v = w_out[l].rearrange("(kc p) n -> p kc n", p=128)
        wctr = [0]

        def load_w(view, c0):
            b = wbuf[wctr[0] % 2]
            wctr[0] += 1
            if os.environ.get("DBGHW"):
                P.dma("sp", b[:, :, :].bitcast(F32), view[:, :, c0:c0 + 256], writes=[b])
            else:
                P.dma("pool", b[:, :, :], view[:, :, c0:c0 + 512], writes=[b])
            return b

        xctr = [0]
        for blk in range(NB):
            tok0 = blk * TBM
            c0 = tok0 // 128
            rope_tables(c0, NT, C_INVA, 8, cosa, sina, rtmp, rtmpi, rtmp2)
            rope_tables(c0, NT, C_INVR, 128, cosr, sinr, rtmp, rtmpi, rtmp2)
            xts = []
            for t in range(NT):
                x_t = xt[xctr[0] % 2]
                xctr[0] += 1
                P.dma("sp", x_t[:, :], src_d[tok0 + t * 128:tok0 + (t + 1) * 128, :], writes=[x_t])
                norm_transpose(x_t, 0, l, hT, t * 128, ps_tr, hb, ss, junk)
            if dbg == 2:
                P.end_phase()
                return
            for cb in range(11):
                wb = load_w(w_in_v, cb * 512)
                DBGSUB = int(os.environ.get("DBGSUB", "0"))
                if DBGSUB == 2:
                    continue
                for t in range(NT):
                    pm = ps_mm[(cb * NT + t) % 2]
                    for k in range(16):
                        mm(P, pm[:, :], hT[:, k, t * 128:(t + 1) * 128], wb[:, k, :], k == 0, k == 15,
                           [hT, wb], [pm])
                    if DBGSUB == 1 or (DBGSUB >= 10 and cb != DBGSUB - 10):
                        continue
                    if cb in (0, 1, 2) or os.environ.get("DBGROPE"):
                        if cb < 2:
                            dst = qa[:, t, cb * 512:(cb + 1) * 512]
                            dstb = qa
                            nh = 8
                        else:
                            dst = kva[:, t, :]
                            dstb = kva
                            nh = 4
                        sg_ = stg[sctr[0] % 2]
                        sctr[0] += 1
                        cp(P, "act", sg_[:, :], pm[:, :], [pm], [sg_])
                        cp(P, "pool", dst, sg_[:, :], [sg_], [dstb])
                        pv = sg_[:, 0:nh * 64].rearrange("p (h d) -> p h d", h=nh)
                        dv = dst[:, 0:nh * 64].rearrange("p (h d) -> p h d", h=nh)
                        x1, x2 = pv[:, :, 0:8], pv[:, :, 8:16]
                        cb_ = cosa[:, t, :].unsqueeze(1).broadcast_to([128, nh, 8])
                        sb_ = sina[:, t, :].unsqueeze(1).broadcast_to([128, nh, 8])
                        a1 = t1[:, 0:nh * 8].rearrange("p (h d) -> p h d", h=nh)
                        a2 = t2[:, 0:nh * 8].rearrange("p (h d) -> p h d", h=nh)
                        a3 = t3[:, 0:nh * 8].rearrange("p (h d) -> p h d", h=nh)
                        tt(P, "dve", a1, x1, cb_, ALU.mult, [sg_, cosa], [t1])
                        tt(P, "dve", a2, x2, sb_, ALU.mult, [sg_, sina], [t2])
                        tt(P, "dve", dv[:, :, 0:8], a1, a2, ALU.subtract, [t1, t2, dstb], [dstb])
                        tt(P, "dve", a1, x2, cb_, ALU.mult, [sg_, cosa], [t1])
                        tt(P, "dve", a3, x1, sb_, ALU.mult, [sg_, sina], [t3])
                        tt(P, "dve", dv[:, :, 8:16], a1, a3, ALU.add, [t1, t3, dstb], [dstb])
                    elif cb in (3, 4, 5, 6):
                        isq = cb < 5
                        dstb = qr if isq else kr
                        hh = (cb - 3) % 2 * 2 if isq else (cb - 5) * 2
                        dst = dstb[:, t, hh * 256:(hh + 2) * 256]
                        sg_ = stg[sctr[0] % 2]
                        sctr[0] += 1
                        cp(P, "act", sg_[:, :], pm[:, :], [pm], [sg_])
                        pv = sg_[:, :].rearrange("p (h d) -> p h d", h=2)
                        x1, x2 = pv[:, :, 0:128], pv[:, :, 128:256]
                        cb_ = cosr[:, t, :].unsqueeze(1).broadcast_to([128, 2, 128])
                        sb_ = sinr[:, t, :].unsqueeze(1).broadcast_to([128, 2, 128])
                        a1 = t1[:, :].rearrange("p (h d) -> p h d", h=2)
                        a2 = t2[:, 0:256].rearrange("p (h d) -> p h d", h=2)
                        a3 = t3[:, 0:256].rearrange("p (h d) -> p h d", h=2)
                        tt(P, "dve", a1[:, :, 0:128], x1, cb_, ALU.mult, [sg_, cosr], [t1])
                        tt(P, "dve", a2, x2, sb_, ALU.mult, [sg_, sinr], [t2])
                        tt(P, "dve", a1[:, :, 0:128], a1[:, :, 0:128], a2, ALU.subtract, [t1, t2], [t1])
                        tt(P, "dve", a1[:, :, 128:256], x2, cb_, ALU.mult, [sg_, cosr], [t1])
                        tt(P, "dve", a3, x1, sb_, ALU.mult, [sg_, sinr], [t3])
                        tt(P, "dve", a1[:, :, 128:256], a1[:, :, 128:256], a3, ALU.add, [t1, t3], [t1])
                        dcol = C_DECQ if isq else C_DECK
                        dec = cst[:, dcol + hh:dcol + hh + 2].unsqueeze(2).broadcast_to([128, 2, 256])
                        tt(P, "dve", dst.rearrange("p (h d) -> p h d", h=2), a1, dec, ALU.mult, [t1, cst], [dstb])
                    elif cb in (7, 8):
                        cp(P, "act", vr[:, t, (cb - 7) * 512:(cb - 6) * 512], pm[:, :], [pm], [vr])
                    else:
                        cp(P, "act", gr[:, t, (cb - 9) * 512:(cb - 8) * 512], pm[:, :], [pm], [gr])
            if dbg == 3:
                P.end_phase()
                return
            for t in range(NT):
                c = c0 + t
                kcur, kprev = kaT[c % 2], kaT[(c + 1) % 2]
                vcur, vprev = va[c % 2], va[(c + 1) % 2]
                for rnd in range(2):
                    for j in range(8):
                        h = rnd * 8 + j
                        tr(P, ps_tr[0:64, j * 128:(j + 1) * 128], qa[:, t, h * 64:(h + 1) * 64], identb[:, :],
                           [qa, identb], [ps_tr], inc=(j == 7))
                    cp(P, "act", qaT[:, rnd * 8:rnd * 8 + 8, :], ps_tr[0:64, :].rearrange("p (h t) -> p h t", h=8),
                       [ps_tr], [qaT])
                for j in range(4):
                    tr(P, ps_tr[0:64, j * 128:(j + 1) * 128], kva[:, t, j * 64:(j + 1) * 64], identb[:, :],
                       [kva, identb], [ps_tr], inc=(j == 3))
                cp(P, "act", kcur[:, :, :], ps_tr[0:64, 0:512].rearrange("p (h t) -> p h t", h=4), [ps_tr], [kcur])
                cp(P, "dve", vcur[:, :], kva[:, t, 256:512], [kva], [vcur])
                first = (c == 0)
                mcol = C_BAND0 if first else C_BAND
                for g in range(4):
                    for hh in range(4):
                        h = g * 4 + hh
                        if not first:
                            mm(P, ps_s[:, hh * 256:hh * 256 + 128], qaT[:, h, :], kprev[:, g, :], True, True,
                               [qaT, kprev], [ps_s])
                        mm(P, ps_s[:, hh * 256 + 128:hh * 256 + 256], qaT[:, h, :], kcur[:, g, :], True, True,
                           [qaT, kcur], [ps_s])
                    if first:
                        for hh in range(4):
                            stt(P, sm[:, hh, 128:256], ps_s[:, hh * 256 + 128:hh * 256 + 256], 0.125,
                                cst[:, mcol + 128:mcol + 256], ALU.mult, ALU.add, [ps_s, cst], [sm])
                        cp(P, "dve", sm[:, :, 0:128], cst[:, mcol:mcol + 128].unsqueeze(1).broadcast_to([128, 4, 128]),
                           [cst], [sm])
                    else:
                        stt(P, sm[:, :, 0:256], ps_s[:, :].rearrange("p (h k) -> p h k", h=4), 0.125,
                            cst[:, mcol:mcol + 256].unsqueeze(1).broadcast_to([128, 4, 256]), ALU.mult, ALU.add,
                            [ps_s, cst], [sm])
                    cp(P, "dve", sm[:, :, 256], sinks[:, l, g * 4:g * 4 + 4], [sinks, sm], [sm])
                    P.op("dve", lambda e: e.tensor_reduce(out=negm[:, :], in_=sm[:, :, :], axis=AX.X, op=ALU.max,
                                                          negate=True), [sm], [negm])
                    for hh in range(4):
                        act(P, pr[:, hh, :], sm[:, hh, :], AF.Exp, [sm, negm], [pr, den], bias=negm[:, hh:hh + 1],
                            accum_out=den[:, hh:hh + 1])
                    P.op("dve", lambda e: e.reciprocal(out=den[:, :], in_=den[:, :]), [den], [den])
                    for hh in range(4):
                        for kb in range(2):
                            tr(P, ps_tr[:, (hh * 2 + kb) * 128:(hh * 2 + kb + 1) * 128], pr[:, hh, kb * 128:(kb + 1) * 128],
                               identb[:, :], [pr, identb], [ps_tr], inc=(hh == 3 and kb == 1))
                    cp(P, "act", prT[:, :, :], ps_tr[:, :].rearrange("p (a t) -> p a t", a=8), [ps_tr], [prT])
                    for hh in range(4):
                        if not first:
                            mm(P, ps_o[:, hh * 64:(hh + 1) * 64], prT[:, hh * 2, :], vprev[:, g * 64:(g + 1) * 64],
                               True, False, [prT, vprev], [ps_o])
                        mm(P, ps_o[:, hh * 64:(hh + 1) * 64], prT[:, hh * 2 + 1, :], vcur[:, g * 64:(g + 1) * 64],
                           first, True, [prT, vcur], [ps_o])
                    tt(P, "dve", ya[:, g * 256:(g + 1) * 256].rearrange("p (h d) -> p h d", h=4),
                       ps_o[:, 0:256].rearrange("p (h d) -> p h d", h=4),
                       den[:, :].unsqueeze(2).broadcast_to([128, 4, 64]), ALU.mult, [ps_o, den], [ya])
                act(P, junk[:, 0:1024], ya[:, :], AF.Square, [ya], [junk, ss], accum_out=ss[:, 0:1])
                act(P, ss[:, 0:1], ss[:, 0:1], AF.Ln, [ss, eps_t], [ss], scale=1.0 / 1024, bias=eps_t[:, 0:1])
                act(P, ss[:, 0:1], ss[:, 0:1], AF.Exp, [ss], [ss], scale=-0.5)
                ts(P, "dve", mix[:, 0:1024], ya[:, :], ss[:, 0:1], ALU.mult, [ya, ss], [mix])

                if dbg == 4:
                    P.end_phase()
                    return
                for rnd in range(2):
                    for j in range(8):
                        idx = rnd * 8 + j
                        h, s = idx // 4, idx % 4
                        srcb = qr if s < 2 else kr
                        col = h * 256 + (s % 2) * 128
                        tr(P, ps_tr[:, j * 128:(j + 1) * 128], srcb[:, t, col:col + 128], identb[:, :],
                           [srcb, identb], [ps_tr], inc=(j == 7))
                    cp(P, "act", qkT[:, rnd * 8:rnd * 8 + 8, :], ps_tr[:, :].rearrange("p (a t) -> p a t", a=8),
                       [ps_tr], [qkT])
                for h in range(4):
                    for dc in range(2):
                        mm(P, ps_o[:, h * 128:(h + 1) * 128], qkT[:, h * 4 + 2 + dc, :], qkT[:, h * 4 + dc, :],
                           dc == 0, dc == 1, [qkT], [ps_o])
                tt(P, "dve", pT[:, :, :], ps_o[:, :].rearrange("p (h i) -> p h i", h=4),
                   maskT[:, :].unsqueeze(1).broadcast_to([128, 4, 128]), ALU.mult, [ps_o, maskT], [pT])
                for h in range(4):
                    ro = ps_s[:, h * 256:(h + 1) * 256]
                    mm(P, ro, pT[:, h, :], vr[:, t, h * 256:(h + 1) * 256], True, False, [pT, vr], [ps_s])
                    for dc in range(2):
                        mm(P, ro, qkT[:, h * 4 + dc, :], Ub[:, h * 2 + dc, :], False, dc == 1, [qkT, Ub], [ps_s])
                for h in range(4):
                    for dc in range(2):
                        mm(P, ps_d[:, dc * 256:(dc + 1) * 256], kr[:, t, h * 256 + dc * 128:h * 256 + (dc + 1) * 128],
                           vr[:, t, h * 256:(h + 1) * 256], True, True, [kr, vr], [ps_d])
                    stt(P, U[:, h * 2:h * 2 + 2, :], U[:, h * 2:h * 2 + 2, :], g128[h],
                        ps_d[:, :].rearrange("p (a d) -> p a d", a=2), ALU.mult, ALU.add, [U, ps_d], [U])
                    act(P, Ub[:, h * 2:h * 2 + 2, :], U[:, h * 2:h * 2 + 2, :], AF.Copy, [U], [Ub], scale=g128[h])
                for h in range(4):
                    act(P, junk[:, h * 256:(h + 1) * 256], ps_s[:, h * 256:(h + 1) * 256], AF.Square, [ps_s], [junk, ssr],
                        accum_out=ssr[:, h:h + 1])
                act(P, ssr[:, :], ssr[:, :], AF.Ln, [ssr, eps_t], [ssr], scale=1.0 / 256, bias=eps_t[:, 0:1])
                act(P, ssr[:, :], ssr[:, :], AF.Exp, [ssr], [ssr], scale=-0.5)
                act(P, sg[:, :], gr[:, t, :], AF.Silu, [gr], [sg])
                for h in range(4):
                    stt(P, mix[:, 1024 + h * 256:1024 + (h + 1) * 256], ps_s[:, h * 256:(h + 1) * 256], ssr[:, h:h + 1],
                        sg[:, h * 256:(h + 1) * 256], ALU.mult, ALU.mult, [ps_s, ssr, sg], [mix])
                for half in range(2):
                    for k in range(8):
                        kk = half * 8 + k
                        tr(P, ps_tr[:, k * 128:(k + 1) * 128], mix[:, kk * 128:(kk + 1) * 128], identb[:, :],
                           [mix, identb], [ps_tr], inc=(k == 7))
                    tt(P, "dve", hT[:, half * 8:half * 8 + 8, t * 128:(t + 1) * 128],
                       ps_tr[:, :].rearrange("p (k t) -> p k t", k=8),
                       gpre[:, l, 1, half * 8:half * 8 + 8].unsqueeze(2).broadcast_to([128, 8, 128]),
                       ALU.mult, [ps_tr, gpre], [hT])
            if dbg == 5:
                P.end_phase()
                return
            for cb in range(4):
                wb = load_w(w_out_v, cb * 512)
                for t in range(NT):
                    pm = ps_mm[(cb * NT + t) % 2]
                    for k in range(16):
                        mm(P, pm[:, :], hT[:, k, t * 128:(t + 1) * 128], wb[:, k, :], k == 0, k == 15, [hT, wb], [pm])
                    cp(P, "act", fo[t][:, cb * 512:(cb + 1) * 512], pm[:, :], [pm], [fo[t]])
            for t in range(NT):
                x_t = xt[xctr[0] % 2]
                xctr[0] += 1
                P.dma("sp", x_t[:, :], src_d[tok0 + t * 128:tok0 + (t + 1) * 128, :], writes=[x_t])
                f_t = fo[t]
                act(P, junk[:, :], f_t[:, :], AF.Square, [f_t], [junk, ss], accum_out=ss[:, 0:1])
                act(P, ss[:, 0:1], ss[:, 0:1], AF.Ln, [ss, eps_t], [ss], scale=1.0 / D, bias=eps_t[:, 0:1])
                act(P, ss[:, 0:1], ss[:, 0:1], AF.Exp, [ss], [ss], scale=-0.5)
                stt(P, f_t[:, :], f_t[:, :], ss[:, 0:1], gpost[:, :], ALU.mult, ALU.mult, [f_t, ss, gpost], [f_t])
                tt(P, "dve", f_t[:, :], f_t[:, :], x_t[:, :], ALU.add, [f_t, x_t], [f_t])
                P.dma("sp", dst_d[tok0 + t * 128:tok0 + (t + 1) * 128, :], f_t[:, :], reads=[f_t], writes=[dst_d],
                      semkey="d_st")
        P.end_phase()

    def ffn_phase(l, src_d, dst_d):
        P.begin_phase()
        NT = TBF // 128
        NB = T // TBF
        hT = P.sb("f_hT", [128, 16, TBF], BF16)
        xt = [P.sb("f_x%d" % i, [128, D], F32) for i in range(2)]
        hb = P.sb("f_hb", [128, D], BF16)
        junk = hb
        ss = P.sb("f_ss", [128, 1], F32)
        gpost = P.sb("f_gpost", [128, D], F32)
        wu = [P.sb("f_wu%d" % i, [128, 16, 512], BF16) for i in range(2)]
        wd = [P.sb("f_wd%d" % i, [128, 44, 128], BF16) for i in range(2)]
        pT = P.sb("f_pT", [128, 44, TBF], BF16)
        fT = P.sb("f_fT", [128, 16, TBF], F32)
        halo = P.sb("f_halo", [128, 88, 2], F32)
        ya_ = P.sb("f_ya", [128, TBF], F32)
        yg_ = P.sb("f_yg", [128, TBF], F32)
        ga_ = P.sb("f_ga", [128, TBF], F32)
        fo = P.sb("f_fo", [128, D], F32)
        ps_tr = P.ps("fp_tr", [128, 1024], BF16)
        ps_u = [P.ps("fp_u%d" % i, [128, 512], F32) for i in range(2)]
        ps_dn = ps_u
        ps_f = P.ps("fp_f", [128, D], F32)

        P.dma("sp", gpost[:, :], gpost_d[l, 1, :].partition_broadcast(128), writes=[gpost])
        P.op("dve", lambda e: e.memset(halo[:, :, :], 0.0), [], [halo])
        w_up_v = w_up[l].rearrange("(kc p) n -> p kc n", p=128)
        w_dn_v = w_down[l].rearrange("(c p) n -> p c n", p=128)
        uctr = [0]
        dctr = [0]
        xctr = [0]
        for blk in range(NB):
            tok0 = blk * TBF
            for t in range(NT):
                x_t = xt[xctr[0] % 2]
                xctr[0] += 1
                P.dma("sp", x_t[:, :], src_d[tok0 + t * 128:tok0 + (t + 1) * 128, :], writes=[x_t])
                norm_transpose(x_t, 2, l, hT, t * 128, ps_tr, hb, ss, junk)
            for jp in range(22):
                wb = wu[uctr[0] % 2]
                uctr[0] += 1
                P.dma("pool", wb[:, :, 0:256], w_up_v[:, :, jp * 256:(jp + 1) * 256], writes=[wb])
                P.dma("pool", wb[:, :, 256:512], w_up_v[:, :, D_FF + jp * 256:D_FF + (jp + 1) * 256], writes=[wb])
                for sub in range(2):
                    j = jp * 2 + sub
                    ys = []
                    for part in range(2):
                        ft = j + 44 * part
                        pu = ps_u[part]
                        wc = part * 256 + sub * 128
                        for k in range(16):
                            mm(P, pu[:, 0:TBF], wb[:, k, wc:wc + 128], hT[:, k, :], k == 0, k == 15, [wb, hT], [pu])
                        y = ya_ if part == 0 else yg_
                        w0 = convw[:, l, 0, ft:ft + 1]
                        w1 = convw[:, l, 1, ft:ft + 1]
                        w2 = convw[:, l, 2, ft:ft + 1]
                        act(P, y[:, :], pu[:, 0:TBF], AF.Identity, [pu, convw, convb], [y], scale=w2,
                            bias=convb[:, l, ft:ft + 1])
                        stt(P, y[:, 1:TBF], pu[:, 0:TBF - 1], w1, y[:, 1:TBF], ALU.mult, ALU.add, [pu, convw, y], [y])
                        stt(P, y[:, 2:TBF], pu[:, 0:TBF - 2], w0, y[:, 2:TBF], ALU.mult, ALU.add, [pu, convw, y], [y])
                        stt(P, y[:, 0:1], halo[:, ft, 1:2], w1, y[:, 0:1], ALU.mult, ALU.add, [halo, convw, y], [y])
                        stt(P, y[:, 0:2], halo[:, ft, 0:2], w0, y[:, 0:2], ALU.mult, ALU.add, [halo, convw, y], [y])
                        cp(P, "dve", halo[:, ft, :], pu[:, TBF - 2:TBF], [pu, halo], [halo])
                    act(P, ga_[:, :], ya_[:, :], AF.Gelu_apprx_tanh, [ya_], [ga_])
                    tt(P, "dve", pT[:, j, :], ga_[:, :], yg_[:, :], ALU.mult, [ga_, yg_], [pT])
            for dt in range(16):
                wb = wd[dctr[0] % 2]
                dctr[0] += 1
                P.dma("pool", wb[:, :, :], w_dn_v[:, :, dt * 128:(dt + 1) * 128], writes=[wb])
                pd = ps_dn[dt % 2]
                for c in range(44):
                    mm(P, pd[:, 0:TBF], wb[:, c, :], pT[:, c, :], c == 0, c == 43, [wb, pT], [pd])
                cp(P, "act", fT[:, dt, :], pd[:, 0:TBF], [pd], [fT])
            for t in range(NT):
                x_t = xt[xctr[0] % 2]
                xctr[0] += 1
                P.dma("sp", x_t[:, :], src_d[tok0 + t * 128:tok0 + (t + 1) * 128, :], writes=[x_t])
                for dt in range(16):
                    P.op("pe", lambda e, dt=dt, t=t: e.transpose(out=ps_f[:, dt * 128:(dt + 1) * 128],
                                                                in_=fT[:, dt, t * 128:(t + 1) * 128],
                                                                identity=identf[:, :]),
                         [fT, identf], [ps_f], inc=(dt == 15))
                act(P, junk[:, :], ps_f[:, :], AF.Square, [ps_f], [junk, ss], accum_out=ss[:, 0:1])
                act(P, ss[:, 0:1], ss[:, 0:1], AF.Ln, [ss, eps_t], [ss], scale=1.0 / D, bias=eps_t[:, 0:1])
                act(P, ss[:, 0:1], ss[:, 0:1], AF.Exp, [ss], [ss], scale=-0.5)
                stt(P, fo[:, :], ps_f[:, :], ss[:, 0:1], gpost[:, :], ALU.mult, ALU.mult, [ps_f, ss, gpost], [fo])
                tt(P, "dve", fo[:, :], fo[:, :], x_t[:, :], ALU.add, [fo, x_t], [fo])
                P.dma("sp", dst_d[tok0 + t * 128:tok0 + (t + 1) * 128, :], fo[:, :], reads=[fo], writes=[dst_d],
                      semkey="d_st")
        P.end_phase()

    if dbg:
        if dbg >= 2:
            mixer_phase(0, x_in, xa_d)
        P.begin_phase()
        cpt = P.sb("cpt", [128, D], F32)
        for i in range(T // 128):
            P.dma("sp", cpt[:, :], x_in[i * 128:(i + 1) * 128, :], writes=[cpt])
            P.dma("sp", out_d[i * 128:(i + 1) * 128, :], cpt[:, :], reads=[cpt], writes=[out_d], semkey="d_st")
        P.end_phase()
        return P.finish()
    cur = x_in
    for l in range(NL):
        last = (l == NL - 1)
        if do_ffn:
            mixer_phase(l, cur, xa_d)
            ffn_phase(l, xa_d, out_d if last else xb_d)
            cur = xb_d
        else:
            mixer_phase(l, cur, out_d if last else xa_d)
            cur = xa_d
    return P.finish()


def _const_tables():
    import ml_dtypes
    cst = np.zeros((128, 1024), np.float32)
    i = np.arange(128)[:, None]
    j = np.arange(256)[None, :]
    diff = 128 + i - j
    band = (diff >= 0) & (diff < 128)
    cst[:, 0:256] = np.where(band, 0.0, -1e30)
    cst[:, 256:512] = np.where(band & (j >= 128), 0.0, -1e30)
    jj = np.arange(128)[:, None]
    ii = np.arange(128)[None, :]
    cst[:, 512:640] = (ii >= jj).astype(np.float32)
    inva = np.power(np.float32(500000.0), -np.arange(8, dtype=np.float32) / np.float32(8))
    invr = np.power(np.float32(10000.0), -np.arange(128, dtype=np.float32) / np.float32(128))
    cst[:, 640:648] = inva[None, :]
    cst[:, 648:776] = invr[None, :]
    p = np.arange(128, dtype=np.float64)[:, None]
    gam = 1.0 - np.power(2.0, -5.0 - np.arange(4, dtype=np.float64))[None, :]
    cst[:, 776:780] = np.power(gam, p)
    cst[:, 780:784] = np.power(gam, -p) / 16.0
    identb = np.eye(128, dtype=np.float32).astype(ml_dtypes.bfloat16)
    return cst, identb


def _fm(v):
    L = v.shape[0]
    return np.ascontiguousarray(v.reshape(L, -1, 128).transpose(2, 0, 1))


def prepare_inputs(inputs, T):
    cst, identb = _const_tables()
    f32 = lambda a: np.ascontiguousarray(np.asarray(a, dtype=np.float32))
    conv_w = f32(inputs["conv_w"])
    convw = np.ascontiguousarray(conv_w.reshape(DEPTH, 3, 88, 128).transpose(3, 0, 1, 2))
    convb = _fm(f32(inputs["conv_b"]))
    sinks = np.ascontiguousarray(np.broadcast_to(f32(inputs["attn_sinks"])[None], (128, DEPTH, 16)))
    mixgain = np.concatenate([f32(inputs["attn_out_norm"]), f32(inputs["ret_out_norm"]).reshape(DEPTH, 1024)], axis=1)
    gpre = np.ascontiguousarray(np.stack([_fm(f32(inputs["pre_mix_norm"])), _fm(mixgain),
                                          _fm(f32(inputs["pre_ffn_norm"]))], axis=2))
    gpost = np.ascontiguousarray(np.stack([f32(inputs["post_mix_norm"]), f32(inputs["post_ffn_norm"])], axis=1))
    shared = {"w_in": f32(inputs["w_in"]), "w_out": f32(inputs["w_out"]), "w_up": f32(inputs["w_up"]),
              "w_down": f32(inputs["w_down"]), "convw": convw, "convb": convb, "sinks": sinks, "gpre": gpre,
              "gpost": gpost, "cst": cst, "identb": identb}
    return shared


def kernel(**inputs):
    T = SEQ
    x = np.asarray(inputs["x"], dtype=np.float32)
    pos = np.asarray(inputs["positions"]).astype(np.int32)
    shared = prepare_inputs(inputs, T)
    nc = build(T)
    in_maps = []
    for c in range(N_CORES):
        b = c % BATCH
        m = dict(shared)
        m["x"] = np.ascontiguousarray(x[b, :T])
        m["pos"] = np.ascontiguousarray(pos[b, :T].reshape(T // 128, 128).T)
        in_maps.append(m)
    res = run_bass_kernel_spmd(nc, in_maps, core_ids=list(range(N_CORES)))
    out = np.stack([np.asarray(res.results[b]["out"]) for b in range(BATCH)], axis=0)
    return out.astype(np.float32)
```
